# Optimizing a Trainium2 kernel written in Bass

```python
import jax, jax.numpy as jnp
from jax import lax
import numpy as np

D_MODEL = 1024
BATCH = 8
SEQ = 8192
DEPTH = 4

GRID_W = 64
CTX_LEN = 256
N_HEADS = 16
HEAD_DIM = D_MODEL // N_HEADS
NA_ROWS = 8
NA_COLS = 16
CONV_WIDTH = 3
FFN_HIDDEN = -(-8 * D_MODEL // (3 * 256)) * 256
EPS = 1e-6
NEG_INF = -1e30

kernel_name = "hybrid_conv_natten_prefix_dit"


def _rms_norm(x, g):
    xf = x.astype(jnp.float32)
    y = xf * lax.rsqrt(jnp.mean(xf * xf, axis=-1, keepdims=True) + EPS)
    return (y * g.astype(jnp.float32)).astype(x.dtype)


def _modulate(h, shift, scale):
    return h * (1 + scale) + shift


def _swiglu(h, w_in, w_out):
    gate, up = jnp.split(h @ w_in, 2, axis=-1)
    return (jax.nn.silu(gate) * up) @ w_out


def _short_conv_mixer(h, w_in, conv_w, w_out):
    b_gate, c_gate, u = jnp.split(h @ w_in, 3, axis=-1)
    z = c_gate * u
    n = z.shape[1]
    pad = CONV_WIDTH // 2
    zp = jnp.pad(z, ((0, 0), (pad, pad), (0, 0)))
    zc = sum(conv_w[j] * zp[:, j:j + n] for j in range(CONV_WIDTH))
    return (b_gate * zc) @ w_out


def _qkv_heads(h, w_qkv, g_q, g_k):
    q, k, v = jnp.split(h @ w_qkv, 3, axis=-1)
    shp = h.shape[:-1] + (N_HEADS, HEAD_DIM)
    return _rms_norm(q.reshape(shp), g_q), _rms_norm(k.reshape(shp), g_k), v.reshape(shp)


def _neighbourhood_attention(h, hc, w_qkv, g_q, g_k, rpb, w_out, with_ctx_out):
    bsz, n, _ = h.shape
    rows = n // GRID_W
    kh = min(NA_ROWS, rows)
    scale = HEAD_DIM ** -0.5
    q, k, v = _qkv_heads(h, w_qkv, g_q, g_k)
    qc, kc, vc = _qkv_heads(hc, w_qkv, g_q, g_k)
    q_g = q.reshape(bsz, rows, GRID_W, N_HEADS, HEAD_DIM)
    k_g = k.reshape(bsz, rows, GRID_W, N_HEADS, HEAD_DIM)
    v_g = v.reshape(bsz, rows, GRID_W, N_HEADS, HEAD_DIM)

    col = jnp.arange(GRID_W)
    col_start = jnp.clip(col - NA_COLS // 2, 0, GRID_W - NA_COLS)
    col_ok = (col[None, :] >= col_start[:, None]) & (col[None, :] < col_start[:, None] + NA_COLS)
    dc = jnp.clip(col[None, :] - col[:, None] + NA_COLS - 1, 0, 2 * NA_COLS - 2)

    def row_block(r):
        r_start = jnp.clip(r - kh // 2, 0, rows - kh)
        q_r = lax.dynamic_index_in_dim(q_g, r, axis=1, keepdims=False)
        k_r = lax.dynamic_slice_in_dim(k_g, r_start, kh, axis=1)
        v_r = lax.dynamic_slice_in_dim(v_g, r_start, kh, axis=1)
        dr = r_start + jnp.arange(kh) - r + NA_ROWS - 1
        bias = rpb[:, dr][:, :, dc].astype(jnp.float32)
        bias = jnp.where(col_ok[None, None], bias, NEG_INF).transpose(0, 2, 1, 3)
        s_win = jnp.einsum('bqhd,brkhd->bhqrk', q_r, k_r,
                           preferred_element_type=jnp.float32) * scale + bias[None]
        s_ctx = jnp.einsum('bqhd,bchd->bhqc', q_r, kc,
                           preferred_element_type=jnp.float32) * scale
        s = jnp.concatenate([s_win.reshape(bsz, N_HEADS, GRID_W, kh * GRID_W), s_ctx], axis=-1)
        p = jax.nn.softmax(s, axis=-1).astype(v.dtype)
        p_win = p[..., :kh * GRID_W].reshape(bsz, N_HEADS, GRID_W, kh, GRID_W)
        p_ctx = p[..., kh * GRID_W:]
        return (jnp.einsum('bhqrk,brkhd->bqhd', p_win, v_r)
                + jnp.einsum('bhqc,bchd->bqhd', p_ctx, vc))

    o = lax.map(row_block, jnp.arange(rows))
    o = jnp.moveaxis(o, 0, 1).reshape(bsz, n, D_MODEL)
    y = o @ w_out
    if not with_ctx_out:
        return y, None
    sc = jnp.einsum('bqhd,bkhd->bhqk', qc, kc, preferred_element_type=jnp.float32) * scale
    pc = jax.nn.softmax(sc, axis=-1).astype(vc.dtype)
    oc = jnp.einsum('bhqk,bkhd->bqhd', pc, vc).reshape(bsz, hc.shape[1], D_MODEL)
    return y, oc @ w_out


def setup_inputs(seed: int = 0) -> dict:
    key = jax.random.key(seed)
    ks = jax.random.split(key, 18)
    D = D_MODEL
    n_conv = (DEPTH + 1) // 2
    n_attn = DEPTH // 2

    def nrm(k, shape):
        return jax.random.normal(k, shape, jnp.float32)

    def lin(k, shape, fan_in, gain=1.0):
        return nrm(k, shape) * (gain * fan_in ** -0.5)

    def norm_gain(k, shape):
        return 1.0 + 0.05 * nrm(k, shape)

    return {
        "x": nrm(ks[0], (BATCH, SEQ, D)),
        "c": nrm(ks[1], (BATCH, D)),
        "ctx": nrm(ks[2], (BATCH, CTX_LEN, D)),
        "c_ctx": nrm(ks[3], (D,)),
        "w_ada": lin(ks[4], (DEPTH, D, 6 * D), D, 0.5),
        "b_ada": 0.02 * nrm(ks[5], (DEPTH, 6 * D)),
        "norm_mix": norm_gain(ks[6], (DEPTH, D)),
        "norm_ffn": norm_gain(ks[7], (DEPTH, D)),
        "conv_w_in": lin(ks[8], (n_conv, D, 3 * D), D),
        "conv_w": lin(ks[9], (n_conv, CONV_WIDTH, D), CONV_WIDTH),
        "conv_w_out": lin(ks[10], (n_conv, D, D), D),
        "attn_w_qkv": lin(ks[11], (n_attn, D, 3 * D), D),
        "attn_q_norm": norm_gain(ks[12], (n_attn, HEAD_DIM)),
        "attn_k_norm": norm_gain(ks[13], (n_attn, HEAD_DIM)),
        "attn_rpb": 0.02 * nrm(ks[14], (n_attn, N_HEADS, 2 * NA_ROWS - 1, 2 * NA_COLS - 1)),
        "attn_w_out": lin(ks[15], (n_attn, D, D), D),
        "ffn_w_in": lin(ks[16], (DEPTH, D, 2 * FFN_HIDDEN), D),
        "ffn_w_out": lin(ks[17], (DEPTH, FFN_HIDDEN, D), FFN_HIDDEN),
    }


def reference(x, c, ctx, c_ctx, w_ada, b_ada, norm_mix, norm_ffn,
              conv_w_in, conv_w, conv_w_out,
              attn_w_qkv, attn_q_norm, attn_k_norm, attn_rpb, attn_w_out,
              ffn_w_in, ffn_w_out):
    for i in range(DEPTH):
        update_ctx = i < DEPTH - 1
        j = i // 2
        mod = jax.nn.silu(c) @ w_ada[i] + b_ada[i]
        mod_c = jax.nn.silu(c_ctx) @ w_ada[i] + b_ada[i]
        sh1, sc1, g1, sh2, sc2, g2 = jnp.split(mod[:, None, :], 6, axis=-1)
        sh1c, sc1c, g1c, sh2c, sc2c, g2c = jnp.split(mod_c, 6, axis=-1)

        h = _modulate(_rms_norm(x, norm_mix[i]), sh1, sc1)
        hc = _modulate(_rms_norm(ctx, norm_mix[i]), sh1c, sc1c)
        if i % 2 == 0:
            y = _short_conv_mixer(h, conv_w_in[j], conv_w[j], conv_w_out[j])
            yc = _short_conv_mixer(hc, conv_w_in[j], conv_w[j], conv_w_out[j]) if update_ctx else None
        else:
            y, yc = _neighbourhood_attention(h, hc, attn_w_qkv[j], attn_q_norm[j], attn_k_norm[j],
                                             attn_rpb[j], attn_w_out[j], update_ctx)
        x = x + g1 * y
        x = x + g2 * _swiglu(_modulate(_rms_norm(x, norm_ffn[i]), sh2, sc2), ffn_w_in[i], ffn_w_out[i])
        if update_ctx:
            ctx = ctx + g1c * yc
            ctx = ctx + g2c * _swiglu(_modulate(_rms_norm(ctx, norm_ffn[i]), sh2c, sc2c),
                                      ffn_w_in[i], ffn_w_out[i])
    return x
```

```python
import contextlib
import numpy as np
import concourse.bass as bass
import concourse.mybir as mybir
from concourse.bass_utils import run_bass_kernel_spmd

F32 = mybir.dt.float32
BF16 = mybir.dt.bfloat16
AF = mybir.ActivationFunctionType
ALU = mybir.AluOpType
AX = mybir.AxisListType

D = 1024
KC = 8
HID = 2816
NH = 16
DH = 64
GW = 64
NCTX = 256
EPS = 1e-6
NEG = -30000.0

ENGS = ("tensor", "vector", "scalar", "gpsimd", "sync")
CH = 16000
NDS = 24
DQ = ("sync", "gpsimd")


class Prog:
    def __init__(self, nc):
        self.nc = nc
        self.stack = contextlib.ExitStack()
        self._csems = {}
        self._dsems = None
        self.active = None
        self.eng = None
        self.cnt = {e: 0 for e in ENGS}
        self.last_w = {}
        self.readers = {}
        self.dma_rr = {q: 0 for q in DQ}
        self.dma_val = {q: [0] * NDS for q in DQ}
        self.snap = None
        self.rots = []

    def prealloc_sems(self, nchunks):
        for e in ("tensor", "vector", "scalar", "gpsimd"):
            for c in range(nchunks[e]):
                self._csems[(e, c)] = self.stack.enter_context(self.nc.semaphore(f"cs_{e}_{c}"))
        self._dsems = {q: [self.stack.enter_context(self.nc.semaphore(f"ds_{q}_{j}")) for j in range(NDS)] for q in DQ}

    def _csem(self, e, c):
        return self._csems[(e, c)]

    def _save(self):
        self.snap = (dict(self.cnt), dict(self.last_w),
                     {k: dict(v) for k, v in self.readers.items()},
                     dict(self.dma_rr), {q: list(v) for q, v in self.dma_val.items()})

    def _restore(self):
        cnt, lw, rd, rr, dv = self.snap
        self.cnt = dict(cnt)
        self.last_w = dict(lw)
        self.readers = {k: {kk: (list(vv) if kk == "dma" else vv) for kk, vv in v.items()} for k, v in rd.items()}
        self.dma_rr = dict(rr)
        self.dma_val = {q: list(v) for q, v in dv.items()}

    def begin_pass(self, name, eng):
        self._restore()
        self.active = name
        self.eng = eng
        self.waited = {}
        for r in self.rots:
            r.i = 0

    def _wait(self, tok):
        if tok[0] == "c":
            _, e, n = tok
            key = ("c", e)
            if self.waited.get(key, -1) >= n:
                return
            self.waited[key] = n
            self.eng.wait_ge(self._csem(e, n // CH), n % CH + 1)
        else:
            _, j, v = tok
            key = ("d", j)
            if self.waited.get(key, -1) >= v:
                return
            self.waited[key] = v
            self.eng.wait_ge(self._dsems[j[0]][j[1]], v)

    def _deps(self, e, reads, writes):
        deps = []
        for k in reads:
            lw = self.last_w.get(k)
            if lw is not None:
                if not (e == "tensor" and lw[0] == "c" and lw[1] == "tensor"):
                    deps.append(lw)
        for k in writes:
            lw = self.last_w.get(k)
            if lw is not None and not (lw[0] == "c" and lw[1] == e):
                deps.append(lw)
            r = self.readers.get(k)
            if r:
                for ek, tv in r.items():
                    if ek == "dma":
                        deps.extend(tv)
                    elif ek != e:
                        deps.append(tv)
        return deps

    def _record(self, tok, reads, writes):
        for k in reads:
            r = self.readers.setdefault(k, {})
            if tok[0] == "c":
                r[tok[1]] = tok
            else:
                r.setdefault("dma", []).append(tok)
        for k in writes:
            self.last_w[k] = tok
            self.readers[k] = {}

    def op(self, e, fn, reads=(), writes=()):
        n = self.cnt[e]
        self.cnt[e] = n + 1
        tok = ("c", e, n)
        if self.active == e:
            for d in self._deps(e, reads, writes):
                self._wait(d)
            ins = fn(self.eng)
            ins.then_inc(self._csem(e, n // CH), 1)
        self._record(tok, reads, writes)
        return tok

    def dma(self, q, fn, reads=(), writes=()):
        jj = self.dma_rr[q]
        self.dma_rr[q] = (jj + 1) % NDS
        prev = self.dma_val[q][jj]
        v = prev + 16
        self.dma_val[q][jj] = v
        j = (q, jj)
        tok = ("d", j, v)
        if self.active == q:
            if prev:
                self._wait(("d", j, prev))
            for d in self._deps("dma:" + q, reads, writes):
                self._wait(d)
            ins = fn(self.eng)
            ins.then_inc(self._dsems[q][jj], 16)
        self._record(tok, reads, writes)
        return tok

    def barrier(self):
        for e in ("tensor", "vector", "scalar", "gpsimd"):
            if self.cnt[e] > 0 and e != self.active:
                self._wait(("c", e, self.cnt[e] - 1))
        for q in DQ:
            for jj in range(NDS):
                if self.dma_val[q][jj]:
                    self._wait(("d", (q, jj), self.dma_val[q][jj]))

    def run_block(self, body):
        self._save()
        with self.nc.Block() as block:
            def mk(name):
                def f(eng):
                    self.begin_pass(name, eng)
                    body()
                    self.barrier()
                return f
            block.tensor(mk("tensor"))
            block.vector(mk("vector"))
            block.scalar(mk("scalar"))
            block.gpsimd(mk("gpsimd"))
            block.sync(mk("sync"))
        self.last_w = {}
        self.readers = {}


class Rot:
    def __init__(self, items):
        self.items = items
        self.i = 0

    def next(self):
        it = self.items[self.i % len(self.items)]
        self.i += 1
        return it


def build(ntok=8192, depth=4, sem_chunks=None):
    nc = bass.Bass("TRN2", target_bir_lowering=False)
    NT = ntok // 128
    NG = ntok // 512
    rows = ntok // GW
    npairs = rows // 2

    def din(name, shape, dt=F32):
        return nc.dram_tensor(name, list(shape), dt, kind="ExternalInput").ap()

    x_in = din("x", [ntok, D])
    ctx_in = din("ctx", [NCTX, D])
    ccols_in = din("ccols", [128, 2, KC])
    w_ada = din("w_ada", [4, D, 6 * D])
    b_ada = din("b_ada", [4, 6 * D])
    normcols_in = din("normcols", [128, 2, 4, KC])
    convw_in = din("convwcols", [128, 2, 3, KC])
    gqk_in = din("gqkcols", [128, 2, 2])
    gqkb_in = din("gqkbc", [128, 2, 2, DH])
    tst_in = din("tst", [2, 128, NH * 14 * GW])
    conv_w_in = din("conv_w_in", [2, D, 3 * D])
    conv_w_out = din("conv_w_out", [2, D, D])
    attn_w_qkv = din("attn_w_qkv", [2, D, 3 * D])
    attn_w_out = din("attn_w_out", [2, D, D])
    ffn_w_in = din("ffn_w_in", [4, D, 2 * HID])
    ffn_w_out = din("ffn_w_out", [4, HID, D])
    out = nc.dram_tensor("out", [ntok, D], F32, kind="ExternalOutput").ap()

    xmid = nc.dram_tensor("xmid", [ntok, D], F32).ap()
    xs = nc.dram_tensor("xs", [ntok, D], F32).ap()
    cwin_bf = nc.dram_tensor("cwin_bf", [2, D, 3 * D], BF16).ap()
    cwout_bf = nc.dram_tensor("cwout_bf", [2, D, D], BF16).ap()
    aqkv_bf = nc.dram_tensor("aqkv_bf", [2, D, 3 * D], BF16).ap()
    aout_bf = nc.dram_tensor("aout_bf", [2, D, D], BF16).ap()
    fwin_bf = nc.dram_tensor("fwin_bf", [4, D, 2 * HID], BF16).ap()
    fwout_bf = nc.dram_tensor("fwout_bf", [4, HID, D], BF16).ap()

    uid = [0]

    def nm(base):
        uid[0] += 1
        return f"{base}_{uid[0]}"

    p = Prog(nc)
    if sem_chunks is None:
        sem_chunks = {"tensor": 10, "vector": 6, "scalar": 6, "gpsimd": 3}
    with p.stack:
        p.prealloc_sems(sem_chunks)

        def sbp(name, shape, dt):
            return p.stack.enter_context(nc.sbuf_tensor(name, list(shape), dt))

        ident = sbp("ident", [128, 128], BF16)
        identf = sbp("identf", [128, 128], F32)
        bdones = sbp("bdones", [128, 128], BF16)
        epsc = sbp("epsc", [128, 1], F32)
        ones_row = sbp("ones_row", [33, 128], F32)
        one11 = sbp("one11", [33, 1], F32)
        silu_l = sbp("silu_l", [128, KC, 33], F32)
        ccols = sbp("ccols_sb", [128, 2, KC], F32)
        normcols = sbp("normcols_sb", [128, 2, 4, KC], F32)
        convw = sbp("convw_sb", [128, 2, 3, KC], F32)
        gqk = sbp("gqk_sb", [128, 2, 2], F32)
        gq8 = sbp("gq8", [128, 2], F32)
        gqkb = sbp("gqkb_sb", [128, 2, 2, DH], F32)
        gmax = sbp("gmax", [128, 4], F32)
        negC = sbp("negC", [128, 2], F32)
        modcols = sbp("modcols", [128, 2, 4, KC], F32)
        Gt = sbp("Gt", [128, 2, 2, D], F32)
        ctxx = sbp("ctxx", [128, 2, D], F32)
        stat = sbp("stat", [128, 8, 4], F32)

        st = Rot([None])
        p.rots.append(st)

        def cast_weights():
            order = []
            for L in range(depth):
                j = L // 2
                if L % 2 == 0:
                    order += [(("cwin", j), cwin_bf[j], conv_w_in[j]), (("cwout", j), cwout_bf[j], conv_w_out[j])]
                else:
                    order += [(("aqkv", j), aqkv_bf[j], attn_w_qkv[j]), (("aout", j), aout_bf[j], attn_w_out[j])]
                order += [(("fwin", L), fwin_bf[L], ffn_w_in[L]), (("fwout", L), fwout_bf[L], ffn_w_out[L])]
            for key, dst, src in order:
                nrow = src.shape[0]
                step = 256
                for r0 in range(0, nrow, step):
                    r1 = min(nrow, r0 + step)
                    p.dma("gpsimd", lambda e: e.dma_start(out=dst[r0:r1, :], in_=src[r0:r1, :]),
                          reads=[], writes=[("wbf", key, r0)])
            return

        wbf_rows = {}

        def wbf_keys(key, nrow):
            return [("wbf", key, r0) for r0 in range(0, nrow, 256)]

        def blk_init():
            p.op("gpsimd", lambda e: e.memset(identf[:, :], 0.0), writes=["identf"])
            p.op("gpsimd", lambda e: e.affine_select(out=identf[:, :], in_=identf[:, :], pattern=[[-1, 128]],
                                                    compare_op=ALU.not_equal, fill=1.0, base=0,
                                                    channel_multiplier=1), reads=["identf"], writes=["identf"])
            p.op("vector", lambda e: e.tensor_copy(out=ident[:, :], in_=identf[:, :]), reads=["identf"], writes=["ident"])
            p.op("vector", lambda e: e.memset(bdones[:, :], 0.0), writes=["bd"])
            p.op("vector", lambda e: e.memset(bdones[0:64, 0:64], 1.0), writes=["bd"])
            p.op("vector", lambda e: e.memset(bdones[64:128, 64:128], 1.0), writes=["bd"])
            p.op("vector", lambda e: e.memset(epsc[:, :], EPS), writes=["epsc"])
            p.op("vector", lambda e: e.memset(ones_row[:, :], 1.0), writes=["ones_row"])
            p.op("vector", lambda e: e.memset(one11[:, :], 1.0), writes=["one11"])
            p.op("vector", lambda e: e.memset(silu_l[:, :, :], 0.0), writes=["silu_l"])
            p.dma("sync", lambda e: e.dma_start(out=ccols[:, :, :], in_=ccols_in[:, :, :]), writes=["ccols"])
            p.dma("sync", lambda e: e.dma_start(out=normcols[:, :, :, :], in_=normcols_in[:, :, :, :]), writes=["normcols"])
            p.dma("sync", lambda e: e.dma_start(out=convw[:, :, :, :], in_=convw_in[:, :, :, :]), writes=["convw"])
            p.dma("sync", lambda e: e.dma_start(out=gqk[:, :, :], in_=gqk_in[:, :, :]), writes=["gqk"])
            p.dma("sync", lambda e: e.dma_start(out=gqkb[:, :, :, :], in_=gqkb_in[:, :, :, :]), writes=["gqkb"])
            p.dma("sync", lambda e: e.dma_start(out=ctxx[:, :, :], in_=ctx_in.rearrange("(t p) d -> p t d", p=128)),
                  writes=[("ctxx", 0), ("ctxx", 1)])
            p.op("scalar", lambda e: e.activation(out=silu_l[:, :, 0], in_=ccols[:, 0, :], func=AF.Silu),
                 reads=["ccols", "silu_l"], writes=["silu_l"])
            p.op("scalar", lambda e: e.activation(out=silu_l[:, :, 32], in_=ccols[:, 1, :], func=AF.Silu),
                 reads=["ccols", "silu_l"], writes=["silu_l"])
            p.op("vector", lambda e: e.tensor_scalar(out=gq8[:, :], in0=gqk[:, :, 0], scalar1=0.125, scalar2=None,
                                                     op0=ALU.mult), reads=["gqk"], writes=["gq8"])
            p.op("vector", lambda e: e.tensor_reduce(out=gmax[:, :], in_=gqkb[:, :, :, :].rearrange("p l t d -> p (l t) d"),
                                                     axis=AX.X, op=ALU.max, apply_absolute_value=True),
                 reads=["gqkb"], writes=["gmax"])
            for l in range(2):
                p.op("vector", lambda e: e.scalar_tensor_tensor(out=negC[:, l:l + 1], in0=gmax[:, 2 * l:2 * l + 1], scalar=-8.0,
                                                                in1=gmax[:, 2 * l + 1:2 * l + 2], op0=ALU.mult, op1=ALU.mult),
                     reads=["gmax"], writes=["negC"])
            cast_weights()

        p.run_block(blk_init)

        def ada_block(L):
            with contextlib.ExitStack() as ps:
                def sb(name, shape, dt):
                    return ps.enter_context(nc.sbuf_tensor(nm(name), list(shape), dt))

                def pst(name, shape, dt=F32):
                    return ps.enter_context(nc.psum_tensor(nm(name), list(shape), dt))
                wch = [sb(f"ada_w{i}", [128, KC, 512], F32) for i in range(2)]
                brow = [sb(f"ada_b{i}", [33, 512], F32) for i in range(2)]
                mrow = [sb(f"ada_m{i}", [33, 512], F32) for i in range(2)]
                mm = [pst(f"ada_mm{i}", [128, 512]) for i in range(2)]
                bc = [pst(f"ada_bc{i}", [128, 512]) for i in range(2)]
                colps = pst("ada_col", [128, 2, 4, KC])

                def body():
                    bci = 0
                    for blk in range(12):
                        i = blk % 2
                        kind, half = blk // 2, blk % 2
                        p.dma("sync", lambda e: e.dma_start(
                            out=wch[i][:, :, :],
                            in_=w_ada[L].rearrange("(kc p) n -> p kc n", p=128)[:, :, blk * 512:(blk + 1) * 512]),
                            writes=[("adaw", i)])
                        for prt in (0, 32):
                            p.dma("sync", lambda e: e.dma_start(out=brow[i][prt:prt + 1, :],
                                                                in_=b_ada[L:L + 1, blk * 512:(blk + 1) * 512]),
                                  writes=[("adab", i, prt)])
                        for kc in range(KC):
                            p.op("tensor", lambda e: e.matmul(mm[i][0:33, :], lhsT=silu_l[:, kc, :], rhs=wch[i][:, kc, :],
                                                              start=(kc == 0), stop=(kc == KC - 1)),
                                 reads=["silu_l", ("adaw", i)], writes=[("adamm", i)])
                        for prt in (0, 32):
                            p.op("vector", lambda e: e.tensor_tensor(out=mrow[i][prt:prt + 1, :], in0=mm[i][prt:prt + 1, :],
                                                                     in1=brow[i][prt:prt + 1, :], op=ALU.add),
                                 reads=[("adamm", i), ("adab", i, prt)], writes=[("adam", i, prt)])
                        if kind in (2, 5):
                            g12 = 0 if kind == 2 else 1
                            for s, prt in ((0, 0), (1, 32)):
                                b = bci % 2
                                bci += 1
                                p.op("tensor", lambda e: e.matmul(bc[b][:, :], lhsT=ones_row[prt:prt + 1, :],
                                                                  rhs=mrow[i][prt:prt + 1, :], start=True, stop=True),
                                     reads=["ones_row", ("adam", i, prt)], writes=[("adabc", b)])
                                p.op("scalar", lambda e: e.activation(out=Gt[:, s, g12, half * 512:(half + 1) * 512],
                                                                      in_=bc[b][:, :], func=AF.Identity),
                                     reads=[("adabc", b)], writes=[("Gt", s, g12)])
                        else:
                            kk = {0: 0, 1: 1, 3: 2, 4: 3}[kind]
                            for s, prt in ((0, 0), (1, 32)):
                                for cc in range(4):
                                    ch = half * 4 + cc
                                    p.op("tensor", lambda e: e.matmul(colps[:, s, kk, ch:ch + 1],
                                                                      lhsT=mrow[i][prt:prt + 1, cc * 128:(cc + 1) * 128],
                                                                      rhs=one11[prt:prt + 1, 0:1], start=True, stop=True),
                                         reads=["one11", ("adam", i, prt)], writes=["colps"])
                    for s in range(2):
                        for sub, nrm in ((0, 0), (1, 1)):
                            p.op("vector", lambda e: e.scalar_tensor_tensor(
                                out=modcols[:, s, 2 * sub, :], in0=colps[:, s, 2 * sub + 1, :], scalar=1.0,
                                in1=normcols[:, nrm, L, :], op0=ALU.add, op1=ALU.mult),
                                reads=["colps", "normcols"], writes=["modcols"])
                            p.op("vector", lambda e: e.tensor_copy(out=modcols[:, s, 2 * sub + 1, :], in_=colps[:, s, 2 * sub, :]),
                                 reads=["colps"], writes=["modcols"])
                p.run_block(body)

        def next_stat():
            i = st.i % 8
            st.i += 1
            return i

        def stage_a_tile(env, xt, xkey, s, sub, hT, hkey, col0):
            si = next_stat()
            skey = ("stat", si)
            xn, xnkey = env["xn"].next()
            p.op("scalar", lambda e: e.activation(out=xn[:, :], in_=xt, func=AF.Square, accum_out=stat[:, si, 0:1]),
                 reads=[xkey], writes=[skey, xnkey])
            p.op("scalar", lambda e: e.activation(out=stat[:, si, 1:2], in_=stat[:, si, 0:1], func=AF.Sqrt,
                                                  scale=1.0 / D, bias=epsc[:, 0:1]),
                 reads=[skey, "epsc"], writes=[skey])
            p.op("vector", lambda e: e.reciprocal(out=stat[:, si, 2:3], in_=stat[:, si, 1:2]), reads=[skey], writes=[skey])
            p.op("vector", lambda e: e.tensor_scalar(out=xn[:, :], in0=xt, scalar1=stat[:, si, 2:3], scalar2=None,
                                                     op0=ALU.mult), reads=[xkey, skey], writes=[xnkey])
            tp, tpkey = env["TP"].next()
            for cc in range(KC):
                p.op("tensor", lambda e: e.transpose(out=tp[:, cc, :], in_=xn[:, cc * 128:(cc + 1) * 128], identity=ident[:, :]),
                     reads=[xnkey, "ident"], writes=[tpkey])
            for cc in range(KC):
                p.op("scalar", lambda e: e.activation(out=hT[:, cc, col0:col0 + 128], in_=tp[:, cc, :], func=AF.Identity,
                                                      scale=modcols[:, s, 2 * sub, cc:cc + 1],
                                                      bias=modcols[:, s, 2 * sub + 1, cc:cc + 1]),
                     reads=[tpkey, "modcols"], writes=[hkey])

        def stage_z_tile(env, uT, ukeys, ucol0, nkc, wsel, xt, xkey, s, g12):
            for half in range(2):
                wv, wkey = wsel(half)
                y, ykey = env["Y"].next()
                for kc in range(nkc):
                    p.op("tensor", lambda e: e.matmul(y[:, :], lhsT=uT[:, kc, ucol0:ucol0 + 128],
                                                      rhs=wv[:, kc, :],
                                                      start=(kc == 0), stop=(kc == nkc - 1)),
                         reads=list(ukeys) + [wkey], writes=[ykey])
                tmp, tkey = env["tmp"].next()
                p.op("vector", lambda e: e.tensor_tensor(out=tmp[:, :], in0=y[:, :], in1=Gt[:, s, g12, half * 512:(half + 1) * 512],
                                                         op=ALU.mult), reads=[ykey, ("Gt", s, g12)], writes=[tkey])
                p.op("gpsimd", lambda e: e.tensor_tensor(out=xt[:, half * 512:(half + 1) * 512], in0=xt[:, half * 512:(half + 1) * 512],
                                                         in1=tmp[:, :], op=ALU.add), reads=[xkey, tkey], writes=[xkey])

        def resident_w(wout, key):
            return lambda half: (wout[:, :, half * 512:(half + 1) * 512], key)

        def load_x_tile(env, pool, src, t):
            xt, xkey = env[pool].next()
            p.dma("sync", lambda e: e.dma_start(out=xt[:, :], in_=src[t * 128:(t + 1) * 128, :]), writes=[xkey])
            return xt, xkey

        def store_x_tile(xt, xkey, dst, t):
            p.dma("gpsimd", lambda e: e.dma_start(out=dst[t * 128:(t + 1) * 128, :], in_=xt[:, :]), reads=[xkey])

        def ring_load(env, src_ap_fn, wkeys, shape_view):
            slot, skey = env["ring"].next()
            p.dma("sync", lambda e: e.dma_start(out=shape_view(slot), in_=src_ap_fn()), reads=wkeys, writes=[skey])
            return slot, skey

        def make_rot(ps, kind, base, n, shape, dt):
            items = []
            for i in range(n):
                if kind == "sb":
                    t = ps.enter_context(nc.sbuf_tensor(nm(f"{base}{i}"), list(shape), dt))
                else:
                    t = ps.enter_context(nc.psum_tensor(nm(f"{base}{i}"), list(shape), dt))
                items.append((t, (base, i)))
            r = Rot(items)
            p.rots.append(r)
            return r

        def ffn_pass(L, src, dst, do_ctx):
            with contextlib.ExitStack() as ps:
                env = {
                    "xn": make_rot(ps, "sb", "f_xn", 2, [128, D], BF16),
                    "TP": make_rot(ps, "ps", "f_tp", 2, [128, KC, 128], BF16),
                    "MM": make_rot(ps, "ps", "f_mm", 4, [128, 512], F32),
                    "Y": make_rot(ps, "ps", "f_y", 2, [128, 512], F32),
                    "tmp": make_rot(ps, "sb", "f_tmp", 2, [128, 512], F32),
                    "xa": make_rot(ps, "sb", "f_xa", 8, [128, D], F32),
                    "ring": make_rot(ps, "sb", "f_ring", 3, [128, 4096], BF16),
                    "sg": make_rot(ps, "sb", "f_sg", 2, [128, 512], F32),
                }
                hTs = [ps.enter_context(nc.sbuf_tensor(nm(f"f_hT{i}"), [128, KC, 512], BF16)) for i in range(2)]
                actT = ps.enter_context(nc.sbuf_tensor(nm("f_actT"), [128, 22, 512], BF16))
                wout = ps.enter_context(nc.sbuf_tensor(nm("f_wout"), [128, 22, D], BF16))

                groups = []
                if do_ctx:
                    groups.append(("ctx", 0, 2))
                for g in range(NG):
                    groups.append(("lat", g, 4))

                def body():
                    for q in range(0, 22, 4):
                        q1 = min(22, q + 4)
                        p.dma("sync", lambda e: e.dma_start(
                            out=wout[:, q:q1, :],
                            in_=fwout_bf[L][q * 128:q1 * 128, :].rearrange("(kc p) n -> p kc n", p=128)),
                            reads=wbf_keys(("fwout", L), HID), writes=["f_wout"])
                    win_view = fwin_bf[L].rearrange("(kc p) (two n) -> p kc two n", p=128, two=2)

                    xtiles = {}

                    def do_stage_a(gi):
                        kind, g, nt = groups[gi]
                        hT = hTs[gi % 2]
                        tiles = []
                        for t in range(nt):
                            if kind == "ctx":
                                xt, xkey = ctxx[:, t, :], ("ctxx", t)
                            else:
                                xt, xkey = load_x_tile(env, "xa", src, g * 4 + t)
                                xt = xt[:, :]
                            stage_a_tile(env, xt, xkey, 1 if kind == "ctx" else 0, 1, hT, ("f_hT", gi % 2), t * 128)
                            tiles.append((xt, xkey))
                        xtiles[gi] = tiles

                    do_stage_a(0)
                    for gi, (kind, g, nt) in enumerate(groups):
                        N = nt * 128
                        s = 1 if kind == "ctx" else 0
                        hT = hTs[gi % 2]
                        hkey = ("f_hT", gi % 2)
                        for b in range(11):
                            slot, skey = env["ring"].next()
                            svv = slot[:, :].rearrange("p (kc two n) -> p kc two n", kc=KC, two=2)
                            for two in range(2):
                                p.dma("sync", lambda e: e.dma_start(out=svv[:, :, two, :],
                                                                    in_=win_view[:, :, two, b * 256:(b + 1) * 256]),
                                      reads=wbf_keys(("fwin", L), D), writes=[skey])
                            sv = slot[:, :].rearrange("p (kc two n) -> p kc two n", kc=KC, two=2)
                            for jj in range(2):
                                j = 2 * b + jj
                                mg, mgk = env["MM"].next()
                                for kc in range(KC):
                                    p.op("tensor", lambda e: e.matmul(mg[:, 0:N], lhsT=sv[:, kc, 0, jj * 128:(jj + 1) * 128],
                                                                      rhs=hT[:, kc, 0:N], start=(kc == 0), stop=(kc == KC - 1)),
                                         reads=[skey, hkey], writes=[mgk])
                                mu, muk = env["MM"].next()
                                for kc in range(KC):
                                    p.op("tensor", lambda e: e.matmul(mu[:, 0:N], lhsT=sv[:, kc, 1, jj * 128:(jj + 1) * 128],
                                                                      rhs=hT[:, kc, 0:N], start=(kc == 0), stop=(kc == KC - 1)),
                                         reads=[skey, hkey], writes=[muk])
                                sg, sgk = env["sg"].next()
                                p.op("scalar", lambda e: e.activation(out=sg[:, 0:N], in_=mg[:, 0:N], func=AF.Silu),
                                     reads=[mgk], writes=[sgk])
                                p.op("vector", lambda e: e.tensor_tensor(out=actT[:, j, 0:N], in0=sg[:, 0:N], in1=mu[:, 0:N],
                                                                         op=ALU.mult), reads=[sgk, muk], writes=["f_actT"])
                            if b == 5 and gi + 1 < len(groups):
                                do_stage_a(gi + 1)
                        for t in range(nt):
                            xt, xkey = xtiles[gi][t]
                            stage_z_tile(env, actT, ["f_actT"], t * 128, 22, resident_w(wout, "f_wout"), xt, xkey, s, 1)
                            if kind == "lat":
                                store_x_tile(xt, xkey, dst, g * 4 + t)
                        del xtiles[gi]
                p.run_block(body)

        def conv_pass(L, j, src, dst):
            with contextlib.ExitStack() as ps:
                env = {
                    "xn": make_rot(ps, "sb", "c_xn", 2, [128, D], BF16),
                    "TP": make_rot(ps, "ps", "c_tp", 2, [128, KC, 128], BF16),
                    "MM": make_rot(ps, "ps", "c_mm", 4, [128, 512], F32),
                    "Y": make_rot(ps, "ps", "c_y", 2, [128, 512], F32),
                    "tmp": make_rot(ps, "sb", "c_tmp", 2, [128, 512], F32),
                    "xa": make_rot(ps, "sb", "c_xa", 2, [128, D], F32),
                    "xz": make_rot(ps, "sb", "c_xz", 2, [128, D], F32),
                    "ring": make_rot(ps, "sb", "c_ring", 3, [128, 4096], BF16),
                    "acc": make_rot(ps, "sb", "c_acc", 3, [128, 512], F32),
                }
                hTs = [ps.enter_context(nc.sbuf_tensor(nm(f"c_hT{i}"), [128, KC, 512], BF16)) for i in range(2)]
                bTs = [ps.enter_context(nc.sbuf_tensor(nm(f"c_bT{i}"), [128, KC, 512], F32)) for i in range(2)]
                zTs = [ps.enter_context(nc.sbuf_tensor(nm(f"c_zT{i}"), [128, KC, 514], F32)) for i in range(2)]
                csb = ps.enter_context(nc.sbuf_tensor(nm("c_csb"), [128, 4, 512], F32))
                uT = ps.enter_context(nc.sbuf_tensor(nm("c_uT"), [128, KC, 512], BF16))
                wout = ps.enter_context(nc.sbuf_tensor(nm("c_wout"), [128, KC, D], BF16))

                def body():
                    for q in range(0, KC, 4):
                        p.dma("sync", lambda e: e.dma_start(
                            out=wout[:, q:q + 4, :],
                            in_=cwout_bf[j][q * 128:(q + 4) * 128, :].rearrange("(kc p) n -> p kc n", p=128)),
                            reads=wbf_keys(("cwout", j), D), writes=["c_wout"])
                    win_view = cwin_bf[j].rearrange("(kc p) n -> p kc n", p=128)
                    units = [(0, "b", 0), (1, "b", 4), (2, "c", 0), (4, "u", 0), (3, "c", 4), (5, "u", 4)]

                    for seq in ("ctx", "lat"):
                        s = 1 if seq == "ctx" else 0
                        ngr = 1 if seq == "ctx" else NG
                        nt = 2 if seq == "ctx" else 4
                        N = nt * 128

                        def stage_ab(g):
                            par = g % 2
                            hT, hkey = hTs[par], ("c_hT", par)
                            bT, zT = bTs[par], zTs[par]
                            for t in range(nt):
                                if seq == "ctx":
                                    xt, xkey = ctxx[:, t, :], ("ctxx", t)
                                else:
                                    xt, xkey = load_x_tile(env, "xa", src, g * 4 + t)
                                    xt = xt[:, :]
                                stage_a_tile(env, xt, xkey, s, 0, hT, hkey, t * 128)
                            for (blk, kind, m0) in units:
                                slot, skey = ring_load(env, lambda: win_view[:, :, blk * 512:(blk + 1) * 512],
                                                       wbf_keys(("cwin", j), D),
                                                       lambda sl: sl[:, :].rearrange("p (kc n) -> p kc n", kc=KC))
                                sv = slot[:, :].rearrange("p (kc n) -> p kc n", kc=KC)
                                for mm_ in range(4):
                                    m = m0 + mm_
                                    ps_, pk = env["MM"].next()
                                    for kc in range(KC):
                                        p.op("tensor", lambda e: e.matmul(ps_[:, 0:N], lhsT=sv[:, kc, mm_ * 128:(mm_ + 1) * 128],
                                                                          rhs=hT[:, kc, 0:N], start=(kc == 0), stop=(kc == KC - 1)),
                                             reads=[skey, hkey], writes=[pk])
                                    if kind == "b":
                                        p.op("scalar", lambda e: e.activation(out=bT[:, m, 0:N], in_=ps_[:, 0:N], func=AF.Identity),
                                             reads=[pk], writes=[("c_bT", par)])
                                    elif kind == "c":
                                        p.op("scalar", lambda e: e.activation(out=csb[:, mm_, 0:N], in_=ps_[:, 0:N], func=AF.Identity),
                                             reads=[pk], writes=[("c_csb", mm_)])
                                    else:
                                        p.op("vector", lambda e: e.tensor_tensor(out=zT[:, m, 1:N + 1], in0=csb[:, mm_, 0:N],
                                                                                 in1=ps_[:, 0:N], op=ALU.mult),
                                             reads=[("c_csb", mm_), pk], writes=[("c_zT", par)])
                            if g == 0:
                                p.op("gpsimd", lambda e: e.memset(zT[:, :, 0:1], 0.0), writes=[("c_zT", par)])
                            else:
                                zp = zTs[1 - par]
                                p.op("gpsimd", lambda e: e.tensor_copy(out=zT[:, :, 0:1], in_=zp[:, :, N:N + 1]),
                                     reads=[("c_zT", 1 - par)], writes=[("c_zT", par)])
                                p.op("gpsimd", lambda e: e.tensor_copy(out=zp[:, :, N + 1:N + 2], in_=zT[:, :, 1:2]),
                                     reads=[("c_zT", par)], writes=[("c_zT", 1 - par)])
                            if g == ngr - 1:
                                p.op("gpsimd", lambda e: e.memset(zT[:, :, N + 1:N + 2], 0.0), writes=[("c_zT", par)])

                        def stage_b(g):
                            par = g % 2
                            bT, zT = bTs[par], zTs[par]
                            zk, bk = ("c_zT", par), ("c_bT", par)
                            for m in range(KC):
                                a0, a0k = env["acc"].next()
                                p.op("scalar", lambda e: e.activation(out=a0[:, 0:N], in_=zT[:, m, 0:N], func=AF.Identity,
                                                                      scale=convw[:, j, 0, m:m + 1]),
                                     reads=[zk, "convw"], writes=[a0k])
                                a1, a1k = env["acc"].next()
                                p.op("vector", lambda e: e.scalar_tensor_tensor(out=a1[:, 0:N], in0=zT[:, m, 1:N + 1],
                                                                                scalar=convw[:, j, 1, m:m + 1], in1=a0[:, 0:N],
                                                                                op0=ALU.mult, op1=ALU.add),
                                     reads=[zk, "convw", a0k], writes=[a1k])
                                a2, a2k = env["acc"].next()
                                p.op("vector", lambda e: e.scalar_tensor_tensor(out=a2[:, 0:N], in0=zT[:, m, 2:N + 2],
                                                                                scalar=convw[:, j, 2, m:m + 1], in1=a1[:, 0:N],
                                                                                op0=ALU.mult, op1=ALU.add),
                                     reads=[zk, "convw", a1k], writes=[a2k])
                                p.op("vector", lambda e: e.tensor_tensor(out=uT[:, m, 0:N], in0=a2[:, 0:N], in1=bT[:, m, 0:N], op=ALU.mult),
                                     reads=[a2k, bk], writes=["c_uT"])
                            for t in range(nt):
                                if seq == "ctx":
                                    xt, xkey = ctxx[:, t, :], ("ctxx", t)
                                else:
                                    xt, xkey = load_x_tile(env, "xz", src, g * 4 + t)
                                    xt = xt[:, :]
                                stage_z_tile(env, uT, ["c_uT"], t * 128, KC, resident_w(wout, "c_wout"), xt, xkey, s, 0)
                                if seq == "lat":
                                    store_x_tile(xt, xkey, dst, g * 4 + t)

                        for g in range(ngr + 1):
                            if g < ngr:
                                stage_ab(g)
                            if g >= 1:
                                stage_b(g - 1)
                p.run_block(body)

        def attn_pass(L, j, src, dst, update_ctx):
            with contextlib.ExitStack() as ps:
                env = {
                    "xn": make_rot(ps, "sb", "a_xn", 2, [128, D], BF16),
                    "TP": make_rot(ps, "ps", "a_tp", 1, [128, KC, 128], BF16),
                                        "tmp": make_rot(ps, "sb", "a_tmp", 1, [128, 512], F32),
                    "xa": make_rot(ps, "sb", "a_xa", 2, [128, D], F32),
                    "xz": make_rot(ps, "sb", "a_xz", 1, [128, D], F32),
                    "ring": make_rot(ps, "sb", "a_ring", 3, [128, 4096], BF16),
                    "sq": make_rot(ps, "sb", "a_sq", 2, [128, 512], BF16),
                    "rt": make_rot(ps, "sb", "a_rt", 2, [128, 512], F32),
                    "Sp": make_rot(ps, "sb", "a_Sp", 2, [128, 640], F32),
                    "PT": make_rot(ps, "sb", "a_PT", 2, [128, 896], BF16),
                }
                S_sets = [ps.enter_context(nc.psum_tensor(nm(f"a_S{i}"), [128, 1024], F32)) for i in range(2)]
                mmx = [(ps.enter_context(nc.psum_tensor(nm(f"a_mmx{i}"), [128, 512], F32)), ("a_mmx", i)) for i in range(2)]
                env["Y"] = Rot([(t[:, :], k) for t, k in mmx])
                env["MM"] = Rot([(S_sets[0][:, 0:512], ("a_S", 0, 0)), (S_sets[0][:, 512:1024], ("a_S", 0, 1)),
                                 (S_sets[1][:, 0:512], ("a_S", 1, 0)), (S_sets[1][:, 512:1024], ("a_S", 1, 1)),
                                 (mmx[0][0][:, :], mmx[0][1]), (mmx[1][0][:, :], mmx[1][1])])
                p.rots.append(env["Y"])
                p.rots.append(env["MM"])
                O_own = ps.enter_context(nc.psum_tensor(nm("a_O"), [128, 2, 80], F32))
                O_slots = [(O_own[0:64, :, :], ("a_O", 0)),
                           (mmx[1][0][0:64, 0:160].rearrange("p (b c) -> p b c", b=2), mmx[1][1])]
                env["Y"] = Rot([(mmx[0][0][:, :], mmx[0][1])])
                p.rots.append(env["Y"])
                hT = ps.enter_context(nc.sbuf_tensor(nm("a_hT"), [128, KC, 512], BF16))
                qTs = [ps.enter_context(nc.sbuf_tensor(nm(f"a_qT{i}"), [128, KC, 512], BF16)) for i in range(2)]
                kT = ps.enter_context(nc.sbuf_tensor(nm("a_kT"), [128, KC, 12 * 128], BF16))
                Va = ps.enter_context(nc.sbuf_tensor(nm("a_Va"), [128, 12, NH, 66], BF16))
                qTc = ps.enter_context(nc.sbuf_tensor(nm("a_qTc"), [128, KC, 256], BF16))
                kTc = ps.enter_context(nc.sbuf_tensor(nm("a_kTc"), [128, KC, 256], BF16))
                Vc = ps.enter_context(nc.sbuf_tensor(nm("a_Vc"), [128, 2, NH, 66], BF16))
                osb = ps.enter_context(nc.sbuf_tensor(nm("a_osb"), [64, 2, D], BF16))
                oT = ps.enter_context(nc.sbuf_tensor(nm("a_oT"), [128, KC, 128], BF16))
                rden = ps.enter_context(nc.sbuf_tensor(nm("a_rden"), [64, 4, 2], F32))
                tst = ps.enter_context(nc.sbuf_tensor(nm("a_tst"), [128, NH * 14 * GW], BF16))

                def key_tiles(pq):
                    rs = [min(max(2 * pq + b - 4, 0), rows - 8) for b in range(2)]
                    kp_lo = min(rs) // 2
                    kp_hi = (max(rs) + 7) // 2
                    tl = []
                    for kp in range(kp_hi, kp_lo - 1, -1):
                        val = []
                        for b in range(2):
                            ok = [rs[b] <= 2 * kp + a < rs[b] + 8 for a in range(2)]
                            if ok[0] and ok[1]:
                                val.append((0, 128))
                            elif ok[0]:
                                val.append((0, 64))
                            elif ok[1]:
                                val.append((64, 128))
                            else:
                                val.append(None)
                        tl.append((kp, val))
                    u0 = 7 - 2 * (kp_hi - pq)
                    assert 1 <= u0 and u0 + 2 * len(tl) <= 15 and len(tl) <= 5
                    return tl, u0

                def body():
                    wo_view = aout_bf[j].rearrange("(kc p) n -> p kc n", p=128)

                    def stream_wout(half):
                        slot, skey = ring_load(env, lambda: wo_view[:, :, half * 512:(half + 1) * 512],
                                               wbf_keys(("aout", j), D),
                                               lambda sl: sl[:, :].rearrange("p (kc n) -> p kc n", kc=KC))
                        return slot[:, :].rearrange("p (kc n) -> p kc n", kc=KC), skey

                    for q in range(4):
                        w = NH * 14 * GW // 4
                        p.dma("gpsimd", lambda e: e.dma_start(out=tst[:, q * w:(q + 1) * w], in_=tst_in[j][:, q * w:(q + 1) * w]),
                              writes=["a_tst"])
                    p.op("gpsimd", lambda e: e.memset(Va[:, :, :, 64:66], 1.0), writes=["a_Va1"])
                    p.op("gpsimd", lambda e: e.memset(Vc[:, :, :, 64:66], 1.0), writes=["a_Vc1"])
                    win_view = aqkv_bf[j].rearrange("(kc p) n -> p kc n", p=128)

                    def qkv(hkey, N, qdst, qdkey, kdst, kdkey, kcol0, vdst, vkeyfn, vslot0):
                        pend = None
                        for blk in range(6):
                            slot, skey = ring_load(env, lambda: win_view[:, :, blk * 512:(blk + 1) * 512],
                                                   wbf_keys(("aqkv", j), D),
                                                   lambda sl: sl[:, :].rearrange("p (kc n) -> p kc n", kc=KC))
                            sv = slot[:, :].rearrange("p (kc n) -> p kc n", kc=KC)
                            if blk < 4:
                                isq = blk < 2
                                for mm_ in range(4):
                                    m = (blk % 2) * 4 + mm_
                                    ps_, pk = env["MM"].next()
                                    for kc in range(KC):
                                        p.op("tensor", lambda e: e.matmul(ps_[:, 0:N], lhsT=sv[:, kc, mm_ * 128:(mm_ + 1) * 128],
                                                                          rhs=hT[:, kc, 0:N], start=(kc == 0), stop=(kc == KC - 1)),
                                             reads=[skey, hkey], writes=[pk])
                                    sq, sqk = env["sq"].next()
                                    p.op("scalar", lambda e: e.activation(out=sq[:, 0:N], in_=ps_[:, 0:N], func=AF.Square),
                                         reads=[pk], writes=[sqk])
                                    cur = (ps_, pk, sq, sqk, isq, m)
                                    if pend is not None:
                                        finish_qk(pend, N, qdst, qdkey, kdst, kdkey, kcol0)
                                    pend = cur
                            else:
                                if pend is not None:
                                    finish_qk(pend, N, qdst, qdkey, kdst, kdkey, kcol0)
                                    pend = None
                                hh = blk - 4
                                for t in range(N // 128):
                                    ps_, pk = env["MM"].next()
                                    for kc in range(KC):
                                        p.op("tensor", lambda e: e.matmul(ps_[:, :], lhsT=hT[:, kc, t * 128:(t + 1) * 128],
                                                                          rhs=sv[:, kc, :], start=(kc == 0), stop=(kc == KC - 1)),
                                             reads=[skey, hkey], writes=[pk])
                                    p.op("scalar", lambda e: e.activation(
                                        out=vdst[:, vslot0 + t, hh * 8:(hh + 1) * 8, 0:64],
                                        in_=ps_[:, :].rearrange("p (h d) -> p h d", d=DH), func=AF.Identity),
                                        reads=[pk], writes=[vkeyfn(t)])

                    def finish_qk(pend, N, qdst, qdkey, kdst, kdkey, kcol0):
                        ps_, pk, sq, sqk, isq, m = pend
                        ss, ssk = env["MM"].next()
                        p.op("tensor", lambda e: e.matmul(ss[:, 0:N], lhsT=bdones[:, :], rhs=sq[:, 0:N], start=True, stop=True),
                             reads=[sqk, "bd"], writes=[ssk])
                        rt, rtk = env["rt"].next()
                        p.op("scalar", lambda e: e.activation(out=rt[:, 0:N], in_=ss[:, 0:N], func=AF.Sqrt, scale=1.0 / DH,
                                                              bias=epsc[:, 0:1]), reads=[ssk, "epsc"], writes=[rtk])
                        p.op("vector", lambda e: e.reciprocal(out=rt[:, 0:N], in_=rt[:, 0:N]), reads=[rtk], writes=[rtk])
                        if isq:
                            p.op("vector", lambda e: e.scalar_tensor_tensor(out=qdst[:, m, 0:N], in0=ps_[:, 0:N], scalar=gq8[:, j:j + 1],
                                                                            in1=rt[:, 0:N], op0=ALU.mult, op1=ALU.mult),
                                 reads=[pk, rtk, "gq8"], writes=[qdkey])
                        else:
                            p.op("vector", lambda e: e.scalar_tensor_tensor(out=kdst[:, m, kcol0:kcol0 + N], in0=ps_[:, 0:N],
                                                                            scalar=gqk[:, j, 1:2], in1=rt[:, 0:N],
                                                                            op0=ALU.mult, op1=ALU.mult),
                                 reads=[pk, rtk, "gqk"], writes=list(kdkey))

                    def attention_tile(qsrc, qkey, qcol0, lat_tiles, u0, xt, xkey, s):
                        nl = len(lat_tiles)
                        tiles = list(lat_tiles) + [("c", 0), ("c", 1)]
                        ntile = nl + 2
                        ptd = {}

                        def qk(h):
                            m, hp = h // 2, (h % 2) * 64
                            S_ps = S_sets[h % 2]
                            for i, tl in enumerate(tiles):
                                bank = 0 if i < 4 else 1
                                if tl[0] == "c":
                                    lhs = kTc[hp:hp + 64, m, tl[1] * 128:(tl[1] + 1) * 128]
                                    rk = ["a_kTc"]
                                else:
                                    lhs = kT[hp:hp + 64, m, tl[0]:tl[0] + 128]
                                    rk = [tl[1]]
                                p.op("tensor", lambda e: e.matmul(S_ps[:, i * 128:(i + 1) * 128], lhsT=lhs,
                                                                  rhs=qsrc[hp:hp + 64, m, qcol0:qcol0 + 128], start=True, stop=True),
                                     reads=rk + [qkey], writes=[("a_S", h % 2, bank)])

                        def softmax(h):
                            S_ps = S_sets[h % 2]
                            sk = lambda bank: ("a_S", h % 2, bank)
                            pt, ptk = env["PT"].next()
                            ptd[h] = (pt, ptk)
                            if nl:
                                sp, spk = env["Sp"].next()
                                n0 = min(nl, 4) * 128
                                p.op("vector", lambda e: e.tensor_tensor(out=sp[:, 0:n0], in0=S_ps[:, 0:n0],
                                                                         in1=tst[:, (h * 14 + u0 - 1) * GW:(h * 14 + u0 - 1) * GW + n0], op=ALU.add),
                                     reads=[sk(0), "a_tst"], writes=[spk])
                                if nl > 4:
                                    p.op("vector", lambda e: e.tensor_tensor(out=sp[:, 512:640], in0=S_ps[:, 512:640],
                                                                             in1=tst[:, (h * 14 + u0 + 7) * GW:(h * 14 + u0 + 9) * GW], op=ALU.add),
                                         reads=["a_tst"], writes=[spk, sk(1)])
                                p.op("scalar", lambda e: e.activation(out=pt[:, 0:nl * 128], in_=sp[:, 0:nl * 128], func=AF.Exp,
                                                                      bias=negC[:, j:j + 1]), reads=[spk, "negC"], writes=[ptk])
                            rdk = [sk(0)] if nl + 2 <= 4 else ([sk(1)] if nl >= 4 else [sk(0), sk(1)])
                            p.op("scalar", lambda e: e.activation(out=pt[:, nl * 128:ntile * 128], in_=S_ps[:, nl * 128:ntile * 128],
                                                                  func=AF.Exp, bias=negC[:, j:j + 1]), reads=["negC"], writes=[ptk] + rdk)

                        def pv(h):
                            pt, ptk = ptd.pop(h)
                            O_v, okey = O_slots[h % 2]
                            for b in range(2):
                                mmlist = []
                                for i, tl in enumerate(tiles):
                                    if tl[0] == "c":
                                        mmlist.append((i, 0, 128, Vc[:, tl[1], h, 0:65], "a_Vc", "a_Vc1"))
                                    else:
                                        v = tl[4][b]
                                        if v is None:
                                            continue
                                        mmlist.append((i, v[0], v[1], Va[:, tl[2], h, 0:65], tl[3], "a_Va1"))
                                for ii, (i, lo, hi, vap, vk, v1k) in enumerate(mmlist):
                                    p.op("tensor", lambda e: e.matmul(O_v[:, b, 0:65],
                                                                      lhsT=pt[lo:hi, i * 128 + b * 64:i * 128 + b * 64 + 64],
                                                                      rhs=vap[lo:hi, :], start=(ii == 0), stop=(ii == len(mmlist) - 1)),
                                         reads=[ptk, vk, v1k], writes=[okey])
                            ri = h % 4
                            p.op("vector", lambda e: e.reciprocal(out=rden[:, ri, :], in_=O_v[:, :, 64]),
                                 reads=[okey], writes=[("a_rden", ri)])
                            for b in range(2):
                                p.op("vector", lambda e: e.tensor_scalar(out=osb[:, b, h * 64:(h + 1) * 64], in0=O_v[:, b, 0:64],
                                                                         scalar1=rden[:, ri, b:b + 1], scalar2=None, op0=ALU.mult),
                                     reads=[okey, ("a_rden", ri)], writes=["a_osb"])

                        qk(0)
                        for h in range(NH):
                            softmax(h)
                            if h + 1 < NH:
                                qk(h + 1)
                            pv(h)
                        tp, tpkey = env["TP"].next()
                        for b in range(2):
                            for cc in range(KC):
                                p.op("tensor", lambda e: e.transpose(out=tp[:, cc, b * 64:(b + 1) * 64],
                                                                     in_=osb[:, b, cc * 128:(cc + 1) * 128], identity=ident[0:64, 0:64]),
                                     reads=["a_osb", "ident"], writes=[tpkey])
                        p.op("scalar", lambda e: e.activation(out=oT[:, :, :], in_=tp[:, :, :], func=AF.Identity),
                             reads=[tpkey], writes=["a_oT"])
                        stage_z_tile(env, oT, ["a_oT"], 0, KC, stream_wout, xt, xkey, s, 0)

                    for t in range(2):
                        stage_a_tile(env, ctxx[:, t, :], ("ctxx", t), 1, 0, hT, "a_hT", t * 128)
                    qkv("a_hT", 256, qTc, "a_qTc", kTc, ["a_kTc"], 0, Vc, lambda t: "a_Vc", 0)
                    import os as _os
                    if update_ctx and not _os.environ.get('KDBG_NOCTXQ'):
                        for t in range(2):
                            attention_tile(qTc, "a_qTc", t * 128, [], 0, ctxx[:, t, :], ("ctxx", t), 1)

                    def do_qkv(g):
                        for t in range(4):
                            xt, xkey = load_x_tile(env, "xa", src, g * 4 + t)
                            stage_a_tile(env, xt[:, :], xkey, 0, 0, hT, "a_hT", t * 128)
                        sl0 = (g % 3) * 4
                        qkv("a_hT", 512, qTs[g % 2], ("a_qT", g % 2), kT, [("a_kT", sl0 + t) for t in range(4)], sl0 * 128,
                            Va, lambda t: ("a_Va", sl0 + t), sl0)

                    def do_attn(g):
                        for t in range(4):
                            pq = g * 4 + t
                            tl, u0 = key_tiles(pq)
                            lat = []
                            for kp, val in tl:
                                slot = kp % 12
                                lat.append((slot * 128, ("a_kT", slot), slot, ("a_Va", slot), val))
                            xt, xkey = load_x_tile(env, "xz", src, pq)
                            attention_tile(qTs[g % 2], ("a_qT", g % 2), t * 128, lat, u0, xt[:, :], xkey, 0)
                            store_x_tile(xt, xkey, dst, pq)

                    for g in range(NG + 1):
                        if g < NG:
                            do_qkv(g)
                        if g >= 1:
                            do_attn(g - 1)
                p.run_block(body)

        cur = x_in
        for L in range(depth):
            update_ctx = L < depth - 1
            j = L // 2
            last = (L == depth - 1)
            ada_block(L)
            if L % 2 == 0:
                conv_pass(L, j, cur, xmid)
            else:
                attn_pass(L, j, cur, xmid, update_ctx)
            dst = out if last else xs
            ffn_pass(L, xmid, dst, update_ctx)
            cur = dst
        counts = dict(p.cnt)
    return nc, counts


def _cols(v):
    return np.ascontiguousarray(np.asarray(v, np.float32).reshape(KC, 128).T)


def _tst_table(rpb):
    rpb = np.asarray(rpb, np.float32)
    nl = rpb.shape[0]
    a = np.arange(128) // 64
    kc = np.arange(128) % 64
    u = np.arange(1, 15)
    qc = np.arange(64)
    dr = 14 - u[None, :] + a[:, None]
    dr_ok = (dr >= 0) & (dr <= 14)
    dc = kc[:, None] - qc[None, :] + 15
    col_start = np.clip(qc - 8, 0, GW - 16)
    col_ok = (kc[:, None] >= col_start[None, :]) & (kc[:, None] < col_start[None, :] + 16)
    dcc = np.clip(dc, 0, 30)
    drc = np.clip(dr, 0, 14)
    g = rpb[:, :, drc[:, :, None], dcc[:, None, :]]
    ok = dr_ok[:, :, None] & col_ok[:, None, :]
    g = np.where(ok[None, None], g, np.float32(NEG))
    g = np.transpose(g, (0, 2, 1, 3, 4))
    return np.ascontiguousarray(g.reshape(nl, 128, NH * 14 * GW))


_CACHE = {}


def _get_nc(ntok, depth):
    key = (ntok, depth)
    if key not in _CACHE:
        chunks = {"tensor": 14, "vector": 8, "scalar": 8, "gpsimd": 4}
        nc, counts = build(ntok, depth, sem_chunks=chunks)
        for e, c in chunks.items():
            assert counts[e] <= c * CH, (e, counts[e])
        _CACHE[key] = nc
    return _CACHE[key]


def prep_inputs(x, c, ctx, c_ctx, w_ada, b_ada, norm_mix, norm_ffn, conv_w_in, conv_w, conv_w_out,
                attn_w_qkv, attn_q_norm, attn_k_norm, attn_rpb, attn_w_out, ffn_w_in, ffn_w_out):
    f = lambda a: np.ascontiguousarray(np.asarray(a, np.float32))
    B = x.shape[0]
    normcols = np.stack([np.stack([_cols(norm_mix[l]) for l in range(4)], 1),
                         np.stack([_cols(norm_ffn[l]) for l in range(4)], 1)], 1)
    convw = np.stack([np.stack([_cols(conv_w[jj, tap]) for tap in range(3)], 1) for jj in range(2)], 1)
    gq = np.asarray(attn_q_norm, np.float32)
    gk = np.asarray(attn_k_norm, np.float32)
    gqk = np.stack([np.stack([np.tile(gq[l], 2), np.tile(gk[l], 2)], 1) for l in range(2)], 1)
    gqkb = np.broadcast_to(np.stack([np.stack([gq[l], gk[l]], 0) for l in range(2)], 0)[None], (128, 2, 2, DH))
    tst = _tst_table(attn_rpb)
    shared = {
        "w_ada": f(w_ada), "b_ada": f(b_ada), "normcols": f(normcols), "convwcols": f(convw),
        "gqkcols": f(gqk), "gqkbc": f(gqkb), "tst": tst,
        "conv_w_in": f(conv_w_in), "conv_w_out": f(conv_w_out), "attn_w_qkv": f(attn_w_qkv),
        "attn_w_out": f(attn_w_out), "ffn_w_in": f(ffn_w_in), "ffn_w_out": f(ffn_w_out),
    }
    ccc = _cols(c_ctx)
    in_maps = []
    for b in range(B):
        m = dict(shared)
        m["x"] = f(x[b])
        m["ctx"] = f(ctx[b])
        m["ccols"] = f(np.stack([_cols(c[b]), ccc], 1))
        in_maps.append(m)
    return in_maps


def kernel(x, c, ctx, c_ctx, w_ada, b_ada, norm_mix, norm_ffn, conv_w_in, conv_w, conv_w_out,
           attn_w_qkv, attn_q_norm, attn_k_norm, attn_rpb, attn_w_out, ffn_w_in, ffn_w_out, _depth=4):
    x = np.asarray(x)
    B, ntok, _ = x.shape
    in_maps = prep_inputs(x, c, ctx, c_ctx, w_ada, b_ada, norm_mix, norm_ffn, conv_w_in, conv_w, conv_w_out,
                          attn_w_qkv, attn_q_norm, attn_k_norm, attn_rpb, attn_w_out, ffn_w_in, ffn_w_out)
    nc = _get_nc(ntok, _depth)
    res = run_bass_kernel_spmd(nc, in_maps, core_ids=list(range(B)))
    return np.stack([np.asarray(r["out"], dtype=np.float32).reshape(ntok, D) for r in res.results], 0)
```
